# Optimizing a Trainium2 kernel written in Bass

```python
import jax
import jax.numpy as jnp
from jax import lax
import numpy as np

D_MODEL = 1024
BATCH = 2
SEQ = 8192
DEPTH = 4

HEAD_DIM = 64
PLE_DIM = 256
ROPE_THETA = 10000.0
BLOCK = 128
N_MIXERS = 3
LN_EPS = 1e-5
DEEPNORM_ALPHA = (2 * DEPTH) ** 0.25
DEEPNORM_BETA = (8 * DEPTH) ** -0.25

A_GROUPS = ((128, 1), (512, 4), (2048, 16))
A_HEADS = D_MODEL // HEAD_DIM
A_WIDTH = A_HEADS * HEAD_DIM
A_IN = 3 * len(A_GROUPS) * A_WIDTH + A_WIDTH

B_HEADS = D_MODEL // HEAD_DIM
B_KV_HEADS = 2
B_WINDOW = 128
B_WIDTH = B_HEADS * HEAD_DIM
B_KV = B_KV_HEADS * HEAD_DIM
B_IN = 2 * B_WIDTH + 2 * B_KV

C_HEADS = D_MODEL // HEAD_DIM
C_KV_HEADS = 4
C_WIDTH = C_HEADS * HEAD_DIM
C_KV = C_KV_HEADS * HEAD_DIM
C_CMP_LEN = 32
C_CMP_STRIDE = 16
C_SEL_LEN = 64
C_N_SEL = 16
C_WINDOW = 512
C_SEL_OVERLAP = (1.0, 2.0, 2.0, 2.0, 1.0)
C_IN = 2 * C_WIDTH + 6 * C_KV + 3 * C_HEADS

N_A_LAYERS = len(range(0, DEPTH, N_MIXERS))
N_B_LAYERS = len(range(1, DEPTH, N_MIXERS))
N_C_LAYERS = len(range(2, DEPTH, N_MIXERS))

kernel_name = 'hybrid_dilated_swa_nsa_deepnorm'


def split_cols(h, sizes):
    out, start = [], 0
    for s in sizes:
        out.append(h[..., start:start + s])
        start += s
    return out


def rope_tables(seq_len):
    inv = 1.0 / (ROPE_THETA ** (jnp.arange(0, HEAD_DIM, 2, dtype=jnp.float32) / HEAD_DIM))
    ang = jnp.arange(seq_len, dtype=jnp.float32)[:, None] * inv[None, :]
    return jnp.cos(ang), jnp.sin(ang)


def apply_rope(t, cos, sin):
    tf = t.astype(jnp.float32)
    half = HEAD_DIM // 2
    t1, t2 = tf[..., :half], tf[..., half:]
    c, s = cos[None, :, None, :], sin[None, :, None, :]
    return jnp.concatenate([t1 * c - t2 * s, t2 * c + t1 * s], axis=-1).astype(t.dtype)


def layer_norm(x, g, b):
    xf = x.astype(jnp.float32)
    mu = jnp.mean(xf, axis=-1, keepdims=True)
    var = jnp.mean(jnp.square(xf - mu), axis=-1, keepdims=True)
    y = (xf - mu) * lax.rsqrt(var + LN_EPS) * g.astype(jnp.float32) + b.astype(jnp.float32)
    return y.astype(x.dtype)


def banded_attention(q, k, v, max_dist, sinks=None):
    B_, L, Hk, G, dh = q.shape
    nb = -(-L // BLOCK)
    pad = nb * BLOCK - L
    npv = -(-max_dist // BLOCK)
    W = (npv + 1) * BLOCK
    qb = jnp.pad(q, ((0, 0), (0, pad), (0, 0), (0, 0), (0, 0))).reshape(B_, nb, BLOCK, Hk, G, dh)

    def windows(t):
        tp = jnp.pad(t, ((0, 0), (npv * BLOCK, pad), (0, 0), (0, 0))).reshape(B_, nb + npv, BLOCK, Hk, dh)
        return jnp.concatenate([tp[:, j:j + nb] for j in range(npv + 1)], axis=2)

    kw, vw = windows(k), windows(v)
    s = jnp.einsum('bnqhgd,bnkhd->bnhgqk', qb, kw).astype(jnp.float32) * (dh ** -0.5)
    dist = np.arange(BLOCK)[:, None] + npv * BLOCK - np.arange(W)[None, :]
    band = (dist >= 0) & (dist <= max_dist)
    kpos = np.arange(nb)[:, None] * BLOCK - npv * BLOCK + np.arange(W)[None, :]
    mask = band[None, :, :] & (kpos >= 0)[:, None, :]
    s = jnp.where(mask[None, :, None, None, :, :], s, -jnp.inf)
    m = jnp.max(s, axis=-1, keepdims=True)
    if sinks is not None:
        sk = sinks.astype(jnp.float32).reshape(1, 1, Hk, G, 1, 1)
        m = jnp.maximum(m, sk)
    e = jnp.exp(s - m)
    l = jnp.sum(e, axis=-1, keepdims=True)
    if sinks is not None:
        l = l + jnp.exp(sk - m)
    o = jnp.einsum('bnhgqk,bnkhd->bnqhgd', e, vw.astype(jnp.float32))
    o = o / jnp.transpose(l, (0, 1, 4, 2, 3, 5))
    lse = jnp.transpose((m + jnp.log(l))[..., 0], (0, 1, 4, 2, 3))
    o = o.reshape(B_, nb * BLOCK, Hk, G, dh)[:, :L]
    lse = lse.reshape(B_, nb * BLOCK, Hk, G)[:, :L]
    return o, lse


def mixer_a(x, w_in, w_out, cos, sin):
    B_, S, _ = x.shape
    h = x @ w_in
    outs, lses = [], []
    for gi, (window, dil) in enumerate(A_GROUPS):
        base = 3 * gi * A_WIDTH
        q, k, v = split_cols(h[..., base:base + 3 * A_WIDTH], [A_WIDTH] * 3)
        q = apply_rope(q.reshape(B_, S, A_HEADS, HEAD_DIM), cos, sin)
        k = apply_rope(k.reshape(B_, S, A_HEADS, HEAD_DIM), cos, sin)
        v = v.reshape(B_, S, A_HEADS, HEAD_DIM)
        Ls = S // dil

        def strided(t):
            return jnp.transpose(t.reshape(B_, Ls, dil, A_HEADS, HEAD_DIM), (0, 2, 1, 3, 4)).reshape(B_ * dil, Ls, A_HEADS, HEAD_DIM)

        o, lse = banded_attention(strided(q)[:, :, :, None, :], strided(k), strided(v), window // dil)
        o = jnp.transpose(o.reshape(B_, dil, Ls, A_HEADS, HEAD_DIM), (0, 2, 1, 3, 4)).reshape(B_, S, A_HEADS, HEAD_DIM)
        lse = jnp.transpose(lse.reshape(B_, dil, Ls, A_HEADS), (0, 2, 1, 3)).reshape(B_, S, A_HEADS)
        outs.append(o)
        lses.append(lse)
    wts = jax.nn.softmax(jnp.stack(lses, axis=0), axis=0)
    o = jnp.einsum('gbsh,gbshd->bshd', wts, jnp.stack(outs, axis=0))
    z = h[..., -A_WIDTH:]
    o = o.reshape(B_, S, A_WIDTH).astype(x.dtype)
    return (o * jax.nn.silu(z)) @ w_out


def mixer_b(x, w_in, sinks, w_out, cos, sin):
    B_, S, _ = x.shape
    G = B_HEADS // B_KV_HEADS
    q, k, v, z = split_cols(x @ w_in, [B_WIDTH, B_KV, B_KV, B_WIDTH])
    q = apply_rope(q.reshape(B_, S, B_HEADS, HEAD_DIM), cos, sin).reshape(B_, S, B_KV_HEADS, G, HEAD_DIM)
    k = apply_rope(k.reshape(B_, S, B_KV_HEADS, HEAD_DIM), cos, sin)
    v = v.reshape(B_, S, B_KV_HEADS, HEAD_DIM)
    o, _ = banded_attention(q, k, v, B_WINDOW - 1, sinks)
    o = o.reshape(B_, S, B_WIDTH).astype(x.dtype)
    return (o * jax.nn.silu(z)) @ w_out


def mixer_c(x, w_in, w_ck, w_cv, pos_cmp, w_out, cos, sin):
    B_, S, _ = x.shape
    Hk, G, dh = C_KV_HEADS, C_HEADS // C_KV_HEADS, HEAD_DIM
    scale = dh ** -0.5
    q, kc, vc, ks, vs, kw, vw, gl, z = split_cols(x @ w_in, [C_WIDTH] + [C_KV] * 6 + [3 * C_HEADS, C_WIDTH])
    q = apply_rope(q.reshape(B_, S, C_HEADS, dh), cos, sin).reshape(B_, S, Hk, G, dh)
    kv_shape = (B_, S, Hk, dh)
    kc = apply_rope(kc.reshape(kv_shape), cos, sin)
    ks = apply_rope(ks.reshape(kv_shape), cos, sin)
    kw = apply_rope(kw.reshape(kv_shape), cos, sin)
    vc, vs, vw = vc.reshape(kv_shape), vs.reshape(kv_shape), vw.reshape(kv_shape)

    n_c = S // C_CMP_STRIDE - 1

    def compress(t, w):
        tr = t.reshape(B_, S // C_CMP_STRIDE, C_CMP_STRIDE, Hk, dh)
        blk = jnp.concatenate([tr[:, :-1], tr[:, 1:]], axis=2)
        return jnp.einsum('bnjhd,jde->bnhe', blk + pos_cmp[None, None, :, None, :], w)

    k_cmp, v_cmp = compress(kc, w_ck), compress(vc, w_cv)
    cmp_end = jnp.arange(n_c) * C_CMP_STRIDE + C_CMP_LEN - 1

    n_s = S // C_SEL_LEN
    n_sel = min(C_N_SEL, n_s)
    ks_blocks = jnp.transpose(ks.reshape(B_, n_s, C_SEL_LEN, Hk, dh), (0, 3, 1, 2, 4))
    vs_blocks = jnp.transpose(vs.reshape(B_, n_s, C_SEL_LEN, Hk, dh), (0, 3, 1, 2, 4))
    bi = jnp.arange(B_)[:, None, None, None]
    hi = jnp.arange(Hk)[None, None, :, None]
    blk_ids = jnp.arange(n_s)

    def query_block(args):
        qb, t0 = args
        tq = t0 + jnp.arange(BLOCK)
        s = jnp.einsum('bqhgd,bnhd->bqhgn', qb, k_cmp).astype(jnp.float32) * scale
        cvalid = cmp_end[None, :] <= tq[:, None]
        s = jnp.where(cvalid[None, :, None, None, :], s, -jnp.inf)
        m = jnp.max(s, axis=-1, keepdims=True)
        m = jnp.where(jnp.isfinite(m), m, 0.0)
        e = jnp.exp(s - m)
        pc = e / jnp.maximum(jnp.sum(e, axis=-1, keepdims=True), 1e-30)
        o_cmp = jnp.einsum('bqhgn,bnhd->bqhgd', pc, v_cmp.astype(jnp.float32))
        imp = jnp.pad(jnp.sum(pc, axis=3), ((0, 0), (0, 0), (0, 0), (1, 1)))
        imp_s = C_SEL_OVERLAP[0] * imp[..., 0::4][..., :n_s]
        for o_off in range(1, len(C_SEL_OVERLAP)):
            imp_s = imp_s + C_SEL_OVERLAP[o_off] * imp[..., o_off::4][..., :n_s]
        cur = tq // C_SEL_LEN
        forced = (blk_ids[None, :] == 0) | (blk_ids[None, :] == cur[:, None]) | (blk_ids[None, :] == cur[:, None] - 1)
        bvalid = blk_ids[None, :] * C_SEL_LEN <= tq[:, None]
        score = jnp.where(forced[None, :, None, :], 1e4, jnp.where(bvalid[None, :, None, :], imp_s, -1.0))
        _, idx = lax.top_k(score, n_sel)
        ksel = ks_blocks[bi, hi, idx]
        vsel = vs_blocks[bi, hi, idx]
        kpos = idx[..., None] * C_SEL_LEN + jnp.arange(C_SEL_LEN)
        smask = kpos <= tq[None, :, None, None, None]
        ss = jnp.einsum('bqhgd,bqhnkd->bqhgnk', qb, ksel).astype(jnp.float32) * scale
        ss = jnp.where(smask[:, :, :, None], ss, -jnp.inf)
        ps = jax.nn.softmax(ss.reshape(ss.shape[:4] + (n_sel * C_SEL_LEN,)), axis=-1).reshape(ss.shape)
        o_slc = jnp.einsum('bqhgnk,bqhnkd->bqhgd', ps, vsel.astype(jnp.float32))
        return o_cmp, o_slc

    nqb = S // BLOCK
    q_blocks = jnp.transpose(q.reshape(B_, nqb, BLOCK, Hk, G, dh), (1, 0, 2, 3, 4, 5))
    t0s = jnp.arange(nqb, dtype=jnp.int32) * BLOCK
    o_cmp, o_slc = lax.map(query_block, (q_blocks, t0s))
    o_cmp = jnp.transpose(o_cmp, (1, 0, 2, 3, 4, 5)).reshape(B_, S, Hk, G, dh)
    o_slc = jnp.transpose(o_slc, (1, 0, 2, 3, 4, 5)).reshape(B_, S, Hk, G, dh)
    o_win, _ = banded_attention(q, kw, vw, C_WINDOW - 1)
    g = jax.nn.sigmoid(gl.astype(jnp.float32)).reshape(B_, S, 3, Hk, G, 1)
    o = g[:, :, 0] * o_cmp + g[:, :, 1] * o_slc + g[:, :, 2] * o_win
    o = o.reshape(B_, S, C_WIDTH).astype(x.dtype)
    return (o * jax.nn.silu(z)) @ w_out


def setup_inputs(seed: int = 0) -> dict:
    key = jax.random.key(seed)
    ks = jax.random.split(key, 16)

    def nrm(k, shape, scale):
        return jax.random.normal(k, shape, jnp.float32) * scale

    return {
        'x': nrm(ks[0], (BATCH, SEQ, D_MODEL), 1.0),
        'p': nrm(ks[1], (DEPTH, BATCH, SEQ, PLE_DIM), 1.0),
        'a_w_in': nrm(ks[2], (N_A_LAYERS, D_MODEL, A_IN), D_MODEL ** -0.5),
        'a_w_out': nrm(ks[3], (N_A_LAYERS, A_WIDTH, D_MODEL), A_WIDTH ** -0.5 * DEEPNORM_BETA),
        'b_w_in': nrm(ks[4], (N_B_LAYERS, D_MODEL, B_IN), D_MODEL ** -0.5),
        'b_sinks': nrm(ks[5], (N_B_LAYERS, B_HEADS), 1.0),
        'b_w_out': nrm(ks[6], (N_B_LAYERS, B_WIDTH, D_MODEL), B_WIDTH ** -0.5 * DEEPNORM_BETA),
        'c_w_in': nrm(ks[7], (N_C_LAYERS, D_MODEL, C_IN), D_MODEL ** -0.5),
        'c_w_ck': nrm(ks[8], (N_C_LAYERS, C_CMP_LEN, HEAD_DIM, HEAD_DIM), (C_CMP_LEN * HEAD_DIM) ** -0.5),
        'c_w_cv': nrm(ks[9], (N_C_LAYERS, C_CMP_LEN, HEAD_DIM, HEAD_DIM), (C_CMP_LEN * HEAD_DIM) ** -0.5),
        'c_pos': nrm(ks[10], (N_C_LAYERS, C_CMP_LEN, HEAD_DIM), 0.5),
        'c_w_out': nrm(ks[11], (N_C_LAYERS, C_WIDTH, D_MODEL), C_WIDTH ** -0.5 * DEEPNORM_BETA),
        'ln_g': 1.0 + nrm(ks[12], (DEPTH, D_MODEL), 0.05),
        'ln_b': nrm(ks[13], (DEPTH, D_MODEL), 0.05),
        'ple_w_proj': nrm(ks[14], (DEPTH, PLE_DIM, D_MODEL), PLE_DIM ** -0.5),
        'ple_w_gate': nrm(ks[15], (DEPTH, D_MODEL, D_MODEL), D_MODEL ** -0.5),
    }


def reference(x, p, a_w_in, a_w_out, b_w_in, b_sinks, b_w_out, c_w_in, c_w_ck, c_w_cv, c_pos, c_w_out, ln_g, ln_b, ple_w_proj, ple_w_gate):
    cos, sin = rope_tables(x.shape[1])
    for i in range(DEPTH):
        j = i // N_MIXERS
        kind = i % N_MIXERS
        if kind == 0:
            h = mixer_a(x, a_w_in[j], a_w_out[j], cos, sin)
        elif kind == 1:
            h = mixer_b(x, b_w_in[j], b_sinks[j], b_w_out[j], cos, sin)
        else:
            h = mixer_c(x, c_w_in[j], c_w_ck[j], c_w_cv[j], c_pos[j], c_w_out[j], cos, sin)
        x = layer_norm(DEEPNORM_ALPHA * x + h, ln_g[i], ln_b[i])
        gate = jax.nn.sigmoid((x @ ple_w_gate[i]).astype(jnp.float32))
        x = (x.astype(jnp.float32) + gate * (p[i] @ ple_w_proj[i]).astype(jnp.float32)).astype(x.dtype)
    return x
```

```python
import types
import numpy as np
from contextlib import ExitStack
import concourse.bass as bass
import concourse.mybir as mybir
from concourse.bass_utils import run_bass_kernel_spmd

F32 = mybir.dt.float32
BF16 = mybir.dt.bfloat16
AF = mybir.ActivationFunctionType
ALU = mybir.AluOpType
AX = mybir.AxisListType

D_MODEL = 1024
BATCH = 2
SEQ = 8192
DEPTH = 4
HD = 64
ALPHA = (2 * DEPTH) ** 0.25
LN_EPS = 1e-5
NCORES = 8

SAME_ENGINE_SYNC = True


class Buf:
    __slots__ = ("name", "w", "r", "rd", "excl")

    def __init__(self, name, excl=False):
        self.name = name
        self.w = None
        self.r = {}
        self.rd = []
        self.excl = excl


class Op:
    __slots__ = ("eng", "fn", "deps", "needs_inc", "dma", "sem", "val", "snap", "idx")

    def __init__(self, eng, fn, dma):
        self.eng = eng
        self.fn = fn
        self.dma = dma
        self.deps = set()
        self.needs_inc = False
        self.sem = None
        self.val = None
        self.snap = None


class Tile:
    def __init__(self, t, b):
        self.t = t
        self.b = b

    def __getitem__(self, k):
        return self.t[k]


def _freeze(fn):
    if fn is None or fn.__closure__ is None:
        return fn
    cells = []
    for c in fn.__closure__:
        try:
            cells.append(types.CellType(c.cell_contents))
        except ValueError:
            cells.append(c)
    return types.FunctionType(fn.__code__, fn.__globals__, fn.__name__, fn.__defaults__, tuple(cells))


class Sched:
    def __init__(self, nc, stack, n_dma_sems=20):
        self.nc = nc
        self.stack = stack
        self.ops = []
        self.out_dmas = []
        self.n_dma_sems = n_dma_sems
        self.ntile = 0

    def sb(self, shape, dtype, name=None):
        self.ntile += 1
        name = "sb_" + (name or f"t{self.ntile}")
        t = self.stack.enter_context(self.nc.sbuf_tensor(name, list(shape), dtype))
        return Tile(t, Buf(name))

    def ps(self, name=None):
        self.ntile += 1
        name = "ps_" + (name or f"p{self.ntile}")
        t = self.stack.enter_context(self.nc.psum_tensor(name, [128, 512], F32))
        return Tile(t, Buf(name, excl=True))

    def op(self, eng, fn, r=(), w=(), dma=False, out=False):
        fn = _freeze(fn)
        o = Op(eng, fn, dma)
        o.idx = len(self.ops)
        deps = set()
        rb = [x.b if isinstance(x, Tile) else x for x in r]
        wb = [x.b if isinstance(x, Tile) else x for x in w]
        for b in rb:
            if b.w is not None:
                deps.add(b.w)
            if b.excl:
                for e2, x in b.r.items():
                    if e2 != eng:
                        deps.add(x)
        for b in wb:
            if b.w is not None:
                deps.add(b.w)
            deps.update(b.r.values())
            deps.update(b.rd)
        for b in rb:
            if b in wb:
                continue
            if dma:
                b.rd.append(o)
            else:
                b.r[eng] = o
        for b in wb:
            b.w = o
            b.r = {}
            b.rd = []
        deps.discard(o)
        for d in deps:
            if d.eng == eng and not d.dma:
                if eng == "pe" or not SAME_ENGINE_SYNC:
                    continue
            o.deps.add(d)
            d.needs_inc = True
        self.ops.append(o)
        if out:
            self.out_dmas.append(o)
        return o

    def emit(self):
        nc = self.nc
        engs = ["pe", "act", "dve", "pool", "sp"]
        fin = Op("sp", None, False)
        for d in self.out_dmas:
            fin.deps.add(d)
            d.needs_inc = True
        self.ops.append(fin)

        sem_ctx = {}
        esem = {}
        for e in engs:
            esem[e] = self.stack.enter_context(nc.semaphore(f"s_{e}"))
        dsem = {}
        for e in ["sp", "pool", "act"]:
            dsem[e] = [self.stack.enter_context(nc.semaphore(f"d_{e}{i}")) for i in range(self.n_dma_sems)]
        cnt = {e: 0 for e in engs}
        dcnt = {}
        drr = {e: 0 for e in dsem}
        seen = {e: {} for e in engs}
        prog = {e: [] for e in engs}
        for o in self.ops:
            E = o.eng
            sn = seen[E]
            waits = []

            def need(sem, val, snap):
                if sn.get(sem, 0) >= val:
                    return
                waits.append((sem, val))
                if snap:
                    for k, v in snap.items():
                        if sn.get(k, 0) < v:
                            sn[k] = v
                sn[sem] = val

            for d in sorted(o.deps, key=lambda x: -x.idx):
                need(d.sem, d.val, d.snap)
            inc = None
            if o.dma:
                i = drr[E]
                drr[E] = (i + 1) % self.n_dma_sems
                s = dsem[E][i]
                prev = dcnt.get(s, 0)
                if prev:
                    need(s, prev, None)
                o.sem, o.val = s, prev + 16
                dcnt[s] = prev + 16
                inc = (s, 16)
                o.snap = dict(sn)
            elif o.needs_inc:
                cnt[E] += 1
                o.sem, o.val = esem[E], cnt[E]
                inc = (esem[E], 1)
                o.snap = dict(sn)
            prog[E].append((waits, o.fn, inc))

        def runner(e):
            def run(eng):
                for waits, fn, inc in prog[e]:
                    for s, v in waits:
                        eng.wait_ge(s, v)
                    if fn is None:
                        continue
                    ins = fn(eng)
                    if inc is not None:
                        ins.then_inc(inc[0], inc[1])
            return run

        with nc.Block() as block:
            block.tensor(runner("pe"))
            block.scalar(runner("act"))
            block.vector(runner("dve"))
            block.gpsimd(runner("pool"))
            block.sync(runner("sp"))


class Rot:
    def __init__(self, tiles):
        self.tiles = tiles
        self.i = 0

    def next(self):
        t = self.tiles[self.i]
        self.i = (self.i + 1) % len(self.tiles)
        return t


def build_tail(NT=2048):
    nc = bass.Bass("TRN2", target_bir_lowering=False)
    og = nc.dram_tensor("og", [NT, D_MODEL], F32, kind="ExternalInput").ap()
    xin = nc.dram_tensor("x", [NT, D_MODEL], F32, kind="ExternalInput").ap()
    pin = nc.dram_tensor("p", [NT, 256], F32, kind="ExternalInput").ap()
    w_out = nc.dram_tensor("w_out", [D_MODEL, D_MODEL], F32, kind="ExternalInput").ap()
    w_g = nc.dram_tensor("w_g", [D_MODEL, D_MODEL], F32, kind="ExternalInput").ap()
    w_p = nc.dram_tensor("w_p", [256, D_MODEL], F32, kind="ExternalInput").ap()
    lng = nc.dram_tensor("ln_g", [1, D_MODEL], F32, kind="ExternalInput").ap()
    lnb = nc.dram_tensor("ln_b", [1, D_MODEL], F32, kind="ExternalInput").ap()
    ident_d = nc.dram_tensor("ident", [128, 128], F32, kind="ExternalInput").ap()
    y = nc.dram_tensor("y", [NT, D_MODEL], F32, kind="ExternalOutput").ap()

    with ExitStack() as stack:
        S = Sched(nc, stack)
        ntile = NT // 128
        identf = S.sb([128, 128], F32, "identf")
        ident = S.sb([128, 128], BF16, "ident")
        S.op("sp", lambda e: e.dma_start(out=identf[:], in_=ident_d), w=[identf], dma=True)
        S.op("dve", lambda e: e.tensor_copy(ident[:], identf[:]), r=[identf], w=[ident])
        gb = S.sb([128, D_MODEL], F32, "gb")
        bb = S.sb([128, D_MODEL], F32, "bb")
        S.op("sp", lambda e: e.dma_start(out=gb[:], in_=lng.partition_broadcast(128)), w=[gb], dma=True)
        S.op("sp", lambda e: e.dma_start(out=bb[:], in_=lnb.partition_broadcast(128)), w=[bb], dma=True)
        wo = S.sb([128, 8, D_MODEL], BF16, "wo")
        wg = S.sb([128, 8, D_MODEL], BF16, "wg")
        wp = S.sb([128, 2, D_MODEL], BF16, "wp")
        stg = Rot([S.sb([128, D_MODEL], F32, f"stg{i}") for i in range(3)])
        k = 0
        for (wd, ws, nch) in ((w_out, wo, 8), (w_g, wg, 8), (w_p, wp, 2)):
            for c in range(nch):
                st = stg.next()
                S.op("sp", lambda e, st=st, wd=wd, c=c: e.dma_start(out=st[:], in_=wd[c * 128:(c + 1) * 128, :]),
                     w=[st], dma=True)
                eng = ("dve", "pool")[k % 2]
                k += 1
                S.op(eng, lambda e, st=st, ws=ws, c=c: e.tensor_copy(ws[:, c, :], st[:]), r=[st], w=[ws])
        banks = Rot([S.ps(f"bank{i}") for i in range(8)])
        ogt = Rot([S.sb([128, D_MODEL], F32, f"ogt{i}") for i in range(2)])
        xt = Rot([S.sb([128, D_MODEL], F32, f"xt{i}") for i in range(2)])
        pt = Rot([S.sb([128, 256], F32, f"pt{i}") for i in range(2)])
        ogb = Rot([S.sb([128, D_MODEL], BF16, f"ogb{i}") for i in range(2)])
        ogT = Rot([S.sb([128, 8, 128], BF16, f"ogT{i}") for i in range(2)])
        rt = Rot([S.sb([128, D_MODEL], F32, f"rt{i}") for i in range(2)])
        sq = Rot([S.sb([128, D_MODEL], F32, f"sq{i}") for i in range(2)])
        st4 = Rot([S.sb([128, 8], F32, f"st4{i}") for i in range(2)])
        xn = Rot([S.sb([128, D_MODEL], F32, f"xn{i}") for i in range(2)])
        xnb = Rot([S.sb([128, D_MODEL], BF16, f"xnb{i}") for i in range(2)])
        xnT = Rot([S.sb([128, 8, 128], BF16, f"xnT{i}") for i in range(2)])
        pb = Rot([S.sb([128, 256], BF16, f"pb{i}") for i in range(2)])
        pT = Rot([S.sb([128, 2, 128], BF16, f"pT{i}") for i in range(2)])
        gt = Rot([S.sb([128, D_MODEL], F32, f"gt{i}") for i in range(2)])
        ot = Rot([S.sb([128, D_MODEL], F32, f"ot{i}") for i in range(2)])

        def transpose_to(src_b, dstT, nch):
            bk = banks.next()
            bkb = bk.t[:].bitcast(BF16)
            for c in range(nch):
                S.op("pe", lambda e, c=c: e.transpose(bkb[:, c * 128:(c + 1) * 128], src_b[:, c * 128:(c + 1) * 128], ident[:]),
                     r=[src_b, ident], w=[bk])
            S.op("dve", lambda e: e.tensor_copy(dstT[:, 0:nch, :], bkb[:, 0:nch * 128].rearrange("p (c t) -> p c t", c=nch)),
                 r=[bk], w=[dstT])

        for ti in range(ntile):
            rows = slice(ti * 128, (ti + 1) * 128)
            a_og, a_x, a_p = ogt.next(), xt.next(), pt.next()
            S.op("sp", lambda e, a=a_og, rows=rows: e.dma_start(out=a[:], in_=og[rows, :]), w=[a_og], dma=True)
            S.op("sp", lambda e, a=a_x, rows=rows: e.dma_start(out=a[:], in_=xin[rows, :]), w=[a_x], dma=True)
            S.op("sp", lambda e, a=a_p, rows=rows: e.dma_start(out=a[:], in_=pin[rows, :]), w=[a_p], dma=True)
            a_ogb = ogb.next()
            S.op("pool", lambda e, a=a_ogb, s=a_og: e.tensor_copy(a[:], s[:]), r=[a_og], w=[a_ogb])
            a_ogT = ogT.next()
            transpose_to(a_ogb, a_ogT, 8)
            yb = [banks.next(), banks.next()]
            for n in range(2):
                for c in range(8):
                    S.op("pe", lambda e, n=n, c=c, a=a_ogT, bk=yb[n]: e.matmul(
                        bk[:], lhsT=a[:, c, :], rhs=wo[:, c, n * 512:(n + 1) * 512], start=(c == 0), stop=(c == 7)),
                        r=[a_ogT, wo], w=[yb[n]])
            a_r = rt.next()
            for n in range(2):
                S.op("dve", lambda e, n=n, a=a_r, xx=a_x, bk=yb[n]: e.scalar_tensor_tensor(
                    out=a[:, n * 512:(n + 1) * 512], in0=xx[:, n * 512:(n + 1) * 512], scalar=float(ALPHA),
                    in1=bk[:], op0=ALU.mult, op1=ALU.add), r=[a_x, yb[n]], w=[a_r])
            a_st = st4.next()
            a_sq = sq.next()
            S.op("dve", lambda e, a=a_st, rr=a_r: e.reduce_sum(out=a[:, 0:1], in_=rr[:], axis=AX.X), r=[a_r], w=[a_st])
            S.op("dve", lambda e, a=a_st: e.tensor_scalar(out=a[:, 1:2], in0=a[:, 0:1], scalar1=-1.0 / D_MODEL, scalar2=None,
                                                          op0=ALU.mult), r=[a_st], w=[a_st])
            S.op("dve", lambda e, a=a_st, rr=a_r: e.tensor_scalar(out=rr[:], in0=rr[:], scalar1=a[:, 1:2], scalar2=None,
                                                                  op0=ALU.add), r=[a_st, a_r], w=[a_r])
            S.op("pool", lambda e, q=a_sq, rr=a_r: e.tensor_tensor(out=q[:], in0=rr[:], in1=rr[:], op=ALU.mult),
                 r=[a_r], w=[a_sq])
            S.op("dve", lambda e, a=a_st, q=a_sq: e.reduce_sum(out=a[:, 2:3], in_=q[:], axis=AX.X), r=[a_sq], w=[a_st])
            S.op("dve", lambda e, a=a_st: e.tensor_scalar(out=a[:, 3:4], in0=a[:, 2:3], scalar1=1.0 / D_MODEL, scalar2=LN_EPS,
                                                          op0=ALU.mult, op1=ALU.add), r=[a_st], w=[a_st])
            S.op("act", lambda e, a=a_st: e.activation(out=a[:, 4:5], in_=a[:, 3:4], func=AF.Sqrt), r=[a_st], w=[a_st])
            S.op("dve", lambda e, a=a_st: e.reciprocal(out=a[:, 5:6], in_=a[:, 4:5]), r=[a_st], w=[a_st])
            a_xn = xn.next()
            S.op("dve", lambda e, a=a_xn, rr=a_r, s=a_st: e.scalar_tensor_tensor(
                out=a[:], in0=rr[:], scalar=s[:, 5:6], in1=gb[:], op0=ALU.mult, op1=ALU.mult), r=[a_r, a_st, gb], w=[a_xn])
            S.op("pool", lambda e, a=a_xn: e.tensor_tensor(out=a[:], in0=a[:], in1=bb[:], op=ALU.add), r=[a_xn, bb], w=[a_xn])
            a_xnb = xnb.next()
            S.op("act", lambda e, a=a_xnb, s=a_xn: e.copy(out=a[:], in_=s[:]), r=[a_xn], w=[a_xnb])
            a_xnT = xnT.next()
            transpose_to(a_xnb, a_xnT, 8)
            a_pb = pb.next()
            S.op("pool", lambda e, a=a_pb, s=a_p: e.tensor_copy(a[:], s[:]), r=[a_p], w=[a_pb])
            a_pT = pT.next()
            transpose_to(a_pb, a_pT, 2)
            a_g = gt.next()
            a_o = ot.next()
            for n in range(2):
                gbk = banks.next()
                for c in range(8):
                    S.op("pe", lambda e, n=n, c=c, a=a_xnT, bk=gbk: e.matmul(
                        bk[:], lhsT=a[:, c, :], rhs=wg[:, c, n * 512:(n + 1) * 512], start=(c == 0), stop=(c == 7)),
                        r=[a_xnT, wg], w=[gbk])
                S.op("act", lambda e, n=n, a=a_g, bk=gbk: e.activation(out=a[:, n * 512:(n + 1) * 512], in_=bk[:], func=AF.Sigmoid),
                     r=[gbk], w=[a_g])
                pbk = banks.next()
                for c in range(2):
                    S.op("pe", lambda e, n=n, c=c, a=a_pT, bk=pbk: e.matmul(
                        bk[:], lhsT=a[:, c, :], rhs=wp[:, c, n * 512:(n + 1) * 512], start=(c == 0), stop=(c == 1)),
                        r=[a_pT, wp], w=[pbk])
                S.op("dve", lambda e, n=n, a=a_g, bk=pbk: e.tensor_tensor(
                    out=a[:, n * 512:(n + 1) * 512], in0=a[:, n * 512:(n + 1) * 512], in1=bk[:], op=ALU.mult),
                    r=[a_g, pbk], w=[a_g])
            S.op("pool", lambda e, a=a_o, g=a_g, s=a_xn: e.tensor_tensor(out=a[:], in0=g[:], in1=s[:], op=ALU.add),
                 r=[a_g, a_xn], w=[a_o])
            S.op("sp", lambda e, a=a_o, rows=rows: e.dma_start(out=y[rows, :], in_=a[:]), r=[a_o], dma=True, out=True)
        S.emit()
    return nc


A_DILS = (1, 4, 16)


class MixCtx:
    def __init__(self, nc, S_, stack, ncol, ng, seq, rope_nh=8):
        self.nc, self.S, self.seq, self.ncol = nc, S_, seq, ncol
        S = S_
        self.x = nc.dram_tensor("x", [seq, D_MODEL], F32, kind="ExternalInput").ap()
        self.w = nc.dram_tensor("w", [D_MODEL, ncol], F32, kind="ExternalInput").ap()
        self.cs = nc.dram_tensor("cs", [ng, seq // 128, 128, 128], F32, kind="ExternalInput").ap()
        self.ident_d = nc.dram_tensor("ident", [128, 128], F32, kind="ExternalInput").ap()
        self.masks_d = nc.dram_tensor("masks", [3, 128, 128], F32, kind="ExternalInput").ap()
        self.og = nc.dram_tensor("og", [seq, 256], F32, kind="ExternalOutput").ap()
        self.identf = S.sb([128, 128], F32, "identf")
        self.ident = S.sb([128, 128], BF16, "identb")
        S.op("sp", lambda e: e.dma_start(out=self.identf[:], in_=self.ident_d), w=[self.identf], dma=True)
        S.op("dve", lambda e: e.tensor_copy(self.ident[:], self.identf[:]), r=[self.identf], w=[self.ident])
        self.maskf = S.sb([128, 3, 128], F32, "maskf")
        self.mask = S.sb([128, 3, 128], BF16, "maskb")
        for i in range(3):
            S.op("sp", lambda e, i=i: e.dma_start(out=self.maskf[:, i, :], in_=self.masks_d[i]), w=[self.maskf], dma=True)
        S.op("dve", lambda e: e.tensor_copy(self.mask[:], self.maskf[:]), r=[self.maskf], w=[self.mask])
        self.W = S.sb([128, 8, ncol], BF16, "W")
        npiece = -(-ncol // 640)
        pw = -(-ncol // npiece)
        wst = Rot([S.sb([128, pw], F32, f"wst{i}") for i in range(2)])
        k = 0
        for c in range(8):
            for pi in range(npiece):
                st = wst.next()
                c0, c1 = pi * pw, min(ncol, (pi + 1) * pw)
                S.op("sp", lambda e, st=st, c=c, c0=c0, c1=c1: e.dma_start(out=st[:, 0:c1 - c0], in_=self.w[c * 128:(c + 1) * 128, c0:c1]),
                     w=[st], dma=True)
                S.op(("dve", "pool")[k % 2], lambda e, st=st, c=c, c0=c0, c1=c1: e.tensor_copy(self.W[:, c, c0:c1], st[:, 0:c1 - c0]),
                     r=[st], w=[self.W])
                k += 1
        self.xT = S.sb([128, 8, 2048], BF16, "xT")
        self.xTb = [Buf(f"xT{i}") for i in range(16)]
        self.xst = Rot([S.sb([128, D_MODEL], F32, f"xst{i}") for i in range(2)])
        self.xbf = Rot([S.sb([128, D_MODEL], BF16, f"xbf{i}") for i in range(2)])
        self.tbanks = Rot([S.ps(f"tb{i}") for i in range(2)])
        self.pbanks = Rot([S.ps(f"pb{i}") for i in range(2)])
        self.sbanks = Rot([S.ps(f"sbk{i}") for i in range(2)])
        self.obank = Rot([S.ps("obk0")])
        self.fbank = Rot([S.ps("fbk0")])
        self.ropeA = Rot([S.sb([128, rope_nh, 64], F32, f"ropeA{i}") for i in range(2)])
        self.ropeB = Rot([S.sb([128, rope_nh, 64], F32, f"ropeB{i}") for i in range(2)])
        self.cst = Rot([S.sb([128, 128], F32, f"cst{i}") for i in range(3)])
        self.pts = Rot([S.sb([128, 512], BF16, f"pt{i}") for i in range(3)])

    def load_xT(self, sc):
        S = self.S
        for ti in range(16):
            rows = slice(sc * 2048 + ti * 128, sc * 2048 + (ti + 1) * 128)
            xs, xb, bk = self.xst.next(), self.xbf.next(), self.tbanks.next()
            S.op("sp", lambda e, xs=xs, rows=rows: e.dma_start(out=xs[:], in_=self.x[rows, :]), w=[xs], dma=True)
            S.op("pool", lambda e, xs=xs, xb=xb: e.tensor_copy(xb[:], xs[:]), r=[xs], w=[xb])
            bkb = bk.t[:].bitcast(BF16)
            for c in range(8):
                S.op("pe", lambda e, c=c, xb=xb, bkb=bkb: e.transpose(bkb[:, c * 128:(c + 1) * 128], xb[:, c * 128:(c + 1) * 128],
                                                                  self.ident[:]), r=[xb, self.ident], w=[bk])
            S.op("act", lambda e, ti=ti, bkb=bkb: e.copy(out=self.xT[:, :, ti * 128:(ti + 1) * 128],
                                                       in_=bkb.rearrange("p (c t) -> p c t", c=8)), r=[bk], w=[self.xTb[ti]])

    def project(self, tsl, col0, ncols, bank):
        S = self.S
        t0, t1 = tsl.start // 128, (tsl.stop - 1) // 128
        xb = self.xTb[t0:t1 + 1]
        for c in range(8):
            S.op("pe", lambda e, c=c: e.matmul(bank[:, 0:ncols], lhsT=self.xT[:, c, tsl], rhs=self.W[:, c, col0:col0 + ncols],
                                               start=(c == 0), stop=(c == 7)), r=[self.W] + xb, w=[bank])

    def load_cs(self, g, blk):
        S = self.S
        t = self.cst.next()
        S.op("sp", lambda e: e.dma_start(out=t[:], in_=self.cs[g, blk]), w=[t], dma=True)
        return t

    def rope(self, bank, col0, nh, cs_t, outs):
        S = self.S
        src = bank[:, col0:col0 + nh * 64].rearrange("p (h d) -> p h d", h=nh)
        a, b = self.ropeA.next(), self.ropeB.next()
        cc = cs_t[:, 0:64].unsqueeze(1).broadcast_to([128, nh, 64])
        s1 = cs_t[:, 64:96].unsqueeze(1).broadcast_to([128, nh, 32])
        s2 = cs_t[:, 96:128].unsqueeze(1).broadcast_to([128, nh, 32])
        S.op("dve", lambda e: e.tensor_tensor(out=a[:, 0:nh, :], in0=src, in1=cc, op=ALU.mult), r=[bank, cs_t], w=[a])
        S.op("dve", lambda e: e.tensor_tensor(out=b[:, 0:nh, 0:32], in0=src[:, :, 32:64], in1=s1, op=ALU.mult), r=[bank, cs_t], w=[b])
        S.op("dve", lambda e: e.tensor_tensor(out=b[:, 0:nh, 32:64], in0=src[:, :, 0:32], in1=s2, op=ALU.mult), r=[bank, cs_t], w=[b])
        for hs, dst, bufs in outs:
            S.op("pool", lambda e, hs=hs, dst=dst: e.tensor_tensor(out=dst, in0=a[:, hs, :], in1=b[:, hs, :], op=ALU.add),
                 r=[a, b], w=bufs)

    def silu_from_bank(self, bank, col0, ncols, dst, dstbufs):
        S = self.S
        z = bank[:, col0:col0 + ncols]
        S.op("act", lambda e: e.activation(out=dst, in_=z, func=AF.Exp, scale=-1.0), r=[bank], w=dstbufs)
        S.op("dve", lambda e: e.tensor_scalar(out=dst, in0=dst, scalar1=1.0, scalar2=None, op0=ALU.add), r=dstbufs, w=dstbufs)
        S.op("dve", lambda e: e.reciprocal(out=dst, in_=dst), r=dstbufs, w=dstbufs)
        S.op("dve", lambda e: e.tensor_tensor(out=dst, in0=dst, in1=z, op=ALU.mult), r=dstbufs + [bank], w=dstbufs)


def build_mixer_a(seq=SEQ):
    nc = bass.Bass("TRN2", target_bir_lowering=False)
    nsc = seq // 2048
    with ExitStack() as stack:
        S = Sched(nc, stack)
        M = MixCtx(nc, S, stack, 2560, 3, seq, rope_nh=4)
        KT = [S.sb([128, 16, 128], BF16, f"KT{g}") for g in range(3)]
        KTb = [[Buf(f"KT{g}_{j}") for j in range(16)] for g in range(3)]
        V = [S.sb([128, 16, 2, 128], BF16, f"V{g}") for g in range(3)]
        Vb = [[Buf(f"V{g}_{j}") for j in range(16)] for g in range(3)]
        cK = [[S.sb([128, A_DILS[g], 128], BF16, f"cK{hp}{g}") for g in range(3)] for hp in range(2)]
        cV = [[S.sb([128, A_DILS[g], 2, 128], BF16, f"cV{hp}{g}") for g in range(3)] for hp in range(2)]
        for g in range(3):
            S.op("pool", lambda e, g=g: e.memset(V[g][:], 1.0), w=Vb[g])
        acc = S.sb([128, 2, 2048], F32, "acc0")
        sz = S.sb([128, 16, 128], F32, "sz")
        szb = [Buf(f"sz{j}") for j in range(16)]
        rqzs = Rot([S.sb([128, 2, 128], BF16, f"rqz{i}") for i in range(2)])
        rks = Rot([S.sb([128, 128], BF16, f"rk{i}") for i in range(2)])
        for t in rqzs.tiles:
            S.op("pool", lambda e, t=t: e.memset(t[:], 0.0), w=[t])
        qzs = Rot([S.sb([128, 2, 128], BF16, f"qz{i}") for i in range(2)])
        rls = Rot([S.sb([128, 2], F32, f"rl{i}") for i in range(2)])
        ogts = Rot([S.sb([128, 128], F32, f"ogt{i}") for i in range(2)])
        mp = S.sb([128, 2, 128], BF16, "mp")
        S.op("dve", lambda e: e.tensor_copy(mp[:, 0, :], M.mask[:, 1, :]), r=[M.mask], w=[mp])
        S.op("dve", lambda e: e.tensor_copy(mp[:, 1, :], M.mask[:, 0, :]), r=[M.mask], w=[mp])

        for sc in range(nsc):
            M.load_xT(sc)
            for hp in range(2):
                for g in range(3):
                    d = A_DILS[g]
                    for jb in range(16):
                        r_, n_ = jb % d, jb // d
                        start = r_ + d * 128 * n_
                        tsl = slice(start, start + d * 127 + 1, d)
                        col0 = hp * 1280 + (0, 512, 896)[g]
                        ncols = 512 if g == 0 else 384
                        bank = M.pbanks.next()
                        M.project(tsl, col0, ncols, bank)
                        cs_t = M.load_cs(g, sc * 16 + jb)
                        rqz, rk = rqzs.next(), rks.next()
                        qdst = rqz[:].rearrange("p h (g d) -> p (h g) d", g=2)[:, 0::3, :]
                        M.rope(bank, 0, 4, cs_t, [(slice(0, 2), qdst, [rqz]),
                                                   (slice(2, 4), rk[:].rearrange("p (h d) -> p h d", h=2), [rk])])
                        S.op("act", lambda e, g=g, jb=jb, bank=bank: e.copy(
                            out=V[g][:, jb, :, 0:64], in_=bank[:, 256:384].rearrange("p (h d) -> p h d", h=2)),
                            r=[bank], w=[Vb[g][jb]])
                        if g == 0:
                            M.silu_from_bank(bank, 384, 128, sz[:, jb, :], [szb[jb]])
                        tb = M.tbanks.next()
                        tbv = tb.t[:].bitcast(BF16)
                        for i in range(2):
                            S.op("pe", lambda e, i=i, rqz=rqz, tbv=tbv: e.transpose(tbv[:, i * 128:(i + 1) * 128], rqz[:, i, :], M.ident[:]),
                                 r=[rqz, M.ident], w=[tb])
                        S.op("pe", lambda e, rk=rk, tbv=tbv: e.transpose(tbv[:, 256:384], rk[:], M.ident[:]), r=[rk, M.ident], w=[tb])
                        qz = qzs.next()
                        S.op("dve", lambda e, qz=qz, tbv=tbv: e.tensor_copy(qz[:], tbv[:, 0:256].rearrange("p (h t) -> p h t", h=2)),
                             r=[tb], w=[qz])
                        S.op("act", lambda e, g=g, jb=jb, tbv=tbv: e.copy(out=KT[g][:, jb, :], in_=tbv[:, 256:384]),
                             r=[tb], w=[KTb[g][jb]])
                        blocks = []
                        if jb >= d:
                            j2 = jb - d
                            blocks.append((KT[g][:, j2, :], V[g][:, j2], [KTb[g][j2], Vb[g][j2]]))
                        elif sc > 0:
                            blocks.append((cK[hp][g][:, jb, :], cV[hp][g][:, jb], [cK[hp][g], cV[hp][g]]))
                        else:
                            blocks.append(None)
                        blocks.append((KT[g][:, jb, :], V[g][:, jb], [KTb[g][jb], Vb[g][jb]]))
                        sbk = M.sbanks.next()
                        for kbi, blk in enumerate(blocks):
                            if blk is None:
                                continue
                            S.op("pe", lambda e, blk=blk, kbi=kbi, qz=qz, sbk=sbk: e.matmul(
                                sbk[:, kbi * 256:(kbi + 1) * 256], lhsT=blk[0], rhs=qz[:].rearrange("p h t -> p (h t)"),
                                start=True, stop=True), r=[qz] + blk[2], w=[sbk])
                        PT = M.pts.next()
                        k0 = 0 if blocks[0] is not None else 1
                        sv = sbk[:].rearrange("p (k h t) -> p k h t", k=2, h=2)[:, k0:2]
                        pv = PT[:].rearrange("p (k h t) -> p k h t", k=2, h=2)[:, k0:2]
                        mv = mp[:, k0:2, :].unsqueeze(2).broadcast_to([128, 2 - k0, 2, 128])
                        S.op("act", lambda e, sv=sv, pv=pv: e.activation(out=pv, in_=sv, func=AF.Exp, scale=0.125), r=[sbk], w=[PT])
                        S.op("dve", lambda e, pv=pv, mv=mv: e.tensor_tensor(out=pv, in0=pv, in1=mv, op=ALU.mult), r=[PT, mp], w=[PT])
                        ob = M.obank.next()
                        live = [(kbi, blk) for kbi, blk in enumerate(blocks) if blk is not None]
                        for h in range(2):
                            for ii, (kbi, blk) in enumerate(live):
                                cofs = (kbi * 2 + h) * 128
                                S.op("pe", lambda e, h=h, blk=blk, cofs=cofs, PT=PT, ob=ob, ii=ii, nl=len(live): e.matmul(
                                    ob[:, h * 128:(h + 1) * 128], lhsT=blk[1][:, h, :], rhs=PT[:, cofs:cofs + 128],
                                    start=(ii == 0), stop=(ii == nl - 1)), r=[PT] + blk[2], w=[ob])
                        accv = acc[:, :, tsl]
                        obv = ob[:, 0:256].rearrange("p (h t) -> p h t", h=2)
                        if g == 0:
                            S.op("dve", lambda e, accv=accv, obv=obv: e.tensor_copy(accv, obv), r=[ob], w=[acc])
                        else:
                            S.op("dve", lambda e, accv=accv, obv=obv: e.tensor_tensor(out=accv, in0=accv, in1=obv, op=ALU.add),
                                 r=[ob, acc], w=[acc])
                if sc + 1 < nsc:
                    for g in range(3):
                        d = A_DILS[g]
                        S.op("pool", lambda e, g=g, d=d, hp=hp: e.tensor_copy(cK[hp][g][:], KT[g][:, 16 - d:16, :]),
                             r=KTb[g][16 - d:16], w=[cK[hp][g]])
                        S.op("pool", lambda e, g=g, d=d, hp=hp: e.tensor_copy(cV[hp][g][:], V[g][:, 16 - d:16]),
                             r=Vb[g][16 - d:16], w=[cV[hp][g]])
                for jb in range(16):
                    fb = M.fbank.next()
                    for h in range(2):
                        S.op("pe", lambda e, h=h, jb=jb, fb=fb: e.transpose(
                            fb[:, h * 128:(h + 1) * 128], acc[:, h, jb * 128:(jb + 1) * 128], M.identf[:]),
                            r=[acc, M.identf], w=[fb])
                    rl = rls.next()
                    fbv = fb[:, 0:256].rearrange("p (h c) -> p h c", h=2)
                    S.op("dve", lambda e, rl=rl, fbv=fbv: e.reciprocal(out=rl[:].unsqueeze(2), in_=fbv[:, :, 64:65]), r=[fb], w=[rl])
                    ogt = ogts.next()
                    for h in range(2):
                        S.op("dve", lambda e, h=h, jb=jb, ogt=ogt, rl=rl, fb=fb: e.scalar_tensor_tensor(
                            out=ogt[:, h * 64:(h + 1) * 64], in0=fb[:, h * 128:h * 128 + 64], scalar=rl[:, h:h + 1],
                            in1=sz[:, jb, h * 64:(h + 1) * 64], op0=ALU.mult, op1=ALU.mult), r=[fb, rl, szb[jb]], w=[ogt])
                    rows = slice(sc * 2048 + jb * 128, sc * 2048 + (jb + 1) * 128)
                    S.op("sp", lambda e, ogt=ogt, rows=rows, hp=hp: e.dma_start(out=M.og[rows, hp * 128:(hp + 1) * 128], in_=ogt[:]),
                         r=[ogt], dma=True, out=True)
        S.emit()
    return nc


def rope_table(pos):
    inv = (1.0 / (10000.0 ** (np.arange(0, HD, 2, dtype=np.float32) / HD))).astype(np.float32)
    ang = pos.astype(np.float32)[..., None] * inv
    c, s = np.cos(ang).astype(np.float32), np.sin(ang).astype(np.float32)
    return np.concatenate([c, c, -s, s], axis=-1).astype(np.float32)


def block_positions(seq, d):
    nsc = seq // 2048
    sc = np.arange(nsc)[:, None, None]
    jb = np.arange(16)[None, :, None]
    i = np.arange(128)[None, None, :]
    pos = sc * 2048 + (jb % d) + d * 128 * (jb // d) + d * i
    return pos.reshape(nsc * 16, 128)


def const_masks():
    k = np.arange(128)[:, None]
    q = np.arange(128)[None, :]
    return np.stack([(k <= q), (k >= q), (k > q)]).astype(np.float32)


def a_cols(hg):
    cols = []
    for hp in range(2):
        hs = [4 * hg + 2 * hp, 4 * hg + 2 * hp + 1]
        for g in range(3):
            for part in range(3):
                for h in hs:
                    cols.extend(range((3 * g + part) * 1024 + h * 64, (3 * g + part) * 1024 + (h + 1) * 64))
            if g == 0:
                for h in hs:
                    cols.extend(range(9 * 1024 + h * 64, 9 * 1024 + (h + 1) * 64))
    return np.array(cols)


def run_mixer_a(x, w_in, seq=SEQ, nbatch=BATCH, prog=None):
    nc = prog or build_mixer_a(seq)
    cs = np.stack([rope_table(block_positions(seq, d)) for d in A_DILS])
    base = dict(cs=cs, ident=np.eye(128, dtype=np.float32), masks=const_masks())
    in_maps = []
    for c in range(4 * nbatch):
        b, hg = c // 4, c % 4
        in_maps.append(dict(base, x=np.ascontiguousarray(x[b]), w=np.ascontiguousarray(w_in[:, a_cols(hg)])))
    res = run_bass_kernel_spmd(nc, in_maps, core_ids=list(range(4 * nbatch)))
    og = np.zeros((nbatch, seq, D_MODEL), np.float32)
    for c in range(4 * nbatch):
        b, hg = c // 4, c % 4
        og[b, :, hg * 256:(hg + 1) * 256] = res.results[c]["og"]
    return og


def b_cols(hg):
    kv = hg // 2
    cols = list(range(hg * 256, (hg + 1) * 256))
    kc = list(range(1024 + kv * 64, 1024 + (kv + 1) * 64))
    cols += kc + kc
    cols += list(range(1024 + 128 + kv * 64, 1024 + 128 + (kv + 1) * 64))
    cols += list(range(1024 + 256 + hg * 256, 1024 + 256 + (hg + 1) * 256))
    return np.array(cols)


def shared_attend(S, M, qz, blocks, ob):
    nb = len(blocks)
    for bi, (kT, Vb_, mk, bufs) in enumerate(blocks):
        sbk = M.sbanks.next()
        S.op("pe", lambda e, kT=kT, sbk=sbk: e.matmul(sbk[:], lhsT=kT, rhs=qz[:].rearrange("p h t -> p (h t)"), start=True, stop=True),
             r=[qz] + bufs, w=[sbk])
        PT = M.pts.next()
        S.op("act", lambda e, PT=PT, sbk=sbk: e.activation(out=PT[:], in_=sbk[:], func=AF.Exp, scale=0.125), r=[sbk], w=[PT])
        if mk is not None:
            pv = PT[:].rearrange("p (h t) -> p h t", h=4)
            mv = mk.unsqueeze(1).broadcast_to([128, 4, 128])
            S.op("dve", lambda e, pv=pv, mv=mv: e.tensor_tensor(out=pv, in0=pv, in1=mv, op=ALU.mult), r=[PT, M.mask], w=[PT])
        S.op("pe", lambda e, Vb_=Vb_, PT=PT, bi=bi: e.matmul(ob[:], lhsT=Vb_, rhs=PT[:], start=(bi == 0), stop=(bi == nb - 1)),
             r=[PT] + bufs, w=[ob])


def build_mixer_b(seq=SEQ):
    nc = bass.Bass("TRN2", target_bir_lowering=False)
    nsc = seq // 2048
    nblk = seq // 128
    with ExitStack() as stack:
        S = Sched(nc, stack)
        M = MixCtx(nc, S, stack, 704, 1, seq, rope_nh=6)
        sinks_d = nc.dram_tensor("sinks", [1, 4], F32, kind="ExternalInput").ap()
        esink = S.sb([128, 4], F32, "esink")
        S.op("sp", lambda e: e.dma_start(out=esink[:], in_=sinks_d.partition_broadcast(128)), w=[esink], dma=True)
        S.op("act", lambda e: e.activation(out=esink[:], in_=esink[:], func=AF.Exp), r=[esink], w=[esink])
        KT = S.sb([128, nblk, 128], BF16, "KT")
        KTb = [Buf(f"KT{j}") for j in range(nblk)]
        V = S.sb([128, nblk, 128], BF16, "V")
        Vb = [Buf(f"V{j}") for j in range(nblk)]
        S.op("pool", lambda e: e.memset(V[:], 1.0), w=Vb)
        rqzs = Rot([S.sb([128, 4, 128], BF16, f"rqz{i}") for i in range(2)])
        rks = Rot([S.sb([128, 128], BF16, f"rk{i}") for i in range(2)])
        for t in rqzs.tiles:
            S.op("pool", lambda e, t=t: e.memset(t[:], 0.0), w=[t])
        qzs = Rot([S.sb([128, 4, 128], BF16, f"qz{i}") for i in range(2)])
        szs = Rot([S.sb([128, 256], F32, f"sz{i}") for i in range(2)])
        oTs = Rot([S.sb([128, 512], F32, f"oT{i}") for i in range(2)])
        rls = Rot([S.sb([128, 4], F32, f"rl{i}") for i in range(2)])
        ogts = Rot([S.sb([128, 256], F32, f"ogt{i}") for i in range(2)])
        for sc in range(nsc):
            M.load_xT(sc)
            for jb in range(16):
                qb = sc * 16 + jb
                tsl = slice(jb * 128, (jb + 1) * 128)
                bank = M.pbanks.next()
                M.project(tsl, 0, 448, bank)
                bank2 = M.pbanks.next()
                M.project(tsl, 448, 256, bank2)
                cs_t = M.load_cs(0, qb)
                rqz, rk = rqzs.next(), rks.next()
                v8 = rqz[:].rearrange("p h (g d) -> p (h g) d", g=2)
                M.rope(bank, 0, 6, cs_t, [(slice(0, 4, 2), v8[:, 0::4, :], [rqz]), (slice(1, 4, 2), v8[:, 3::4, :], [rqz]),
                                           (slice(4, 6), rk[:].rearrange("p (h d) -> p h d", h=2), [rk])])
                S.op("act", lambda e, qb=qb, bank=bank: e.copy(out=V[:, qb, 0:64], in_=bank[:, 384:448]), r=[bank], w=[Vb[qb]])
                sz = szs.next()
                M.silu_from_bank(bank2, 0, 256, sz[:], [sz])
                tb = M.tbanks.next()
                tbv = tb.t[:].bitcast(BF16)
                for i in range(4):
                    S.op("pe", lambda e, i=i, rqz=rqz, tbv=tbv: e.transpose(tbv[:, i * 128:(i + 1) * 128], rqz[:, i, :], M.ident[:]),
                         r=[rqz, M.ident], w=[tb])
                S.op("pe", lambda e, rk=rk, tbv=tbv: e.transpose(tbv[:, 512:640], rk[:], M.ident[:]), r=[rk, M.ident], w=[tb])
                qz = qzs.next()
                S.op("dve", lambda e, qz=qz, tbv=tbv: e.tensor_copy(qz[:], tbv[:, 0:512].rearrange("p (h t) -> p h t", h=4)),
                     r=[tb], w=[qz])
                S.op("act", lambda e, qb=qb, tbv=tbv: e.copy(out=KT[:, qb, :], in_=tbv[:, 512:640]), r=[tb], w=[KTb[qb]])
                blocks = []
                if qb > 0:
                    blocks.append((KT[:, qb - 1, :], V[:, qb - 1, :], M.mask[:, 2, :], [KTb[qb - 1], Vb[qb - 1]]))
                blocks.append((KT[:, qb, :], V[:, qb, :], M.mask[:, 0, :], [KTb[qb], Vb[qb]]))
                ob = M.obank.next()
                shared_attend(S, M, qz, blocks, ob)
                oT = oTs.next()
                S.op("act", lambda e, oT=oT, ob=ob: e.copy(out=oT[:], in_=ob[:]), r=[ob], w=[oT])
                fb = M.fbank.next()
                for h in range(4):
                    S.op("pe", lambda e, h=h, fb=fb, oT=oT: e.transpose(fb[:, h * 128:(h + 1) * 128], oT[:, h * 128:(h + 1) * 128], M.identf[:]),
                         r=[oT, M.identf], w=[fb])
                rl = rls.next()
                fbv = fb[:].rearrange("p (h c) -> p h c", h=4)
                S.op("dve", lambda e, rl=rl, fbv=fbv: e.tensor_tensor(out=rl[:].unsqueeze(2), in0=fbv[:, :, 64:65], in1=esink[:].unsqueeze(2),
                                                                      op=ALU.add), r=[fb, esink], w=[rl])
                S.op("dve", lambda e, rl=rl: e.reciprocal(out=rl[:], in_=rl[:]), r=[rl], w=[rl])
                ogt = ogts.next()
                for h in range(4):
                    S.op("dve", lambda e, h=h, ogt=ogt, rl=rl, fb=fb, sz=sz: e.scalar_tensor_tensor(
                        out=ogt[:, h * 64:(h + 1) * 64], in0=fb[:, h * 128:h * 128 + 64], scalar=rl[:, h:h + 1],
                        in1=sz[:, h * 64:(h + 1) * 64], op0=ALU.mult, op1=ALU.mult), r=[fb, rl, sz], w=[ogt])
                rows = slice(qb * 128, (qb + 1) * 128)
                S.op("sp", lambda e, ogt=ogt, rows=rows: e.dma_start(out=M.og[rows, :], in_=ogt[:]), r=[ogt], dma=True, out=True)
        S.emit()
    return nc


def mixer_in_maps(kind, x, w_in, seq, nbatch, extra=None):
    if kind == "a":
        cs = np.stack([rope_table(block_positions(seq, d)) for d in A_DILS])
        colf = a_cols
    else:
        cs = rope_table(block_positions(seq, 1))[None]
        colf = b_cols if kind == "b" else c_cols
    base = dict(cs=cs, ident=np.eye(128, dtype=np.float32), masks=const_masks())
    in_maps = []
    for c in range(4 * nbatch):
        b, hg = c // 4, c % 4
        m = dict(base, x=np.ascontiguousarray(x[b]), w=np.ascontiguousarray(w_in[:, colf(hg)]))
        if extra is not None:
            m.update(extra(b, hg))
        in_maps.append(m)
    return in_maps


def gather_og(res, seq, nbatch):
    og = np.zeros((nbatch, seq, D_MODEL), np.float32)
    for c in range(4 * nbatch):
        b, hg = c // 4, c % 4
        og[b, :, hg * 256:(hg + 1) * 256] = res.results[c]["og"]
    return og


def c_cols(hg):
    def blk(base, w=64):
        return list(range(base + hg * w, base + (hg + 1) * w))
    kc, vc, ks, vs, kw, vw = [blk(1024 + 256 * i) for i in range(6)]
    cols = kc + ks + ks + kw + kw + vc + vs + vw
    cols += blk(0, 256)
    for br in range(3):
        cols += list(range(2560 + br * 16 + hg * 4, 2560 + br * 16 + hg * 4 + 4))
    cols += blk(2608, 256)
    return np.array(cols)


def c_consts(seq):
    nblk = seq // 128
    n_ = np.arange(512)[:, None]
    j_ = np.arange(128)[None, :]
    msel = np.where((n_ == 4 * j_) | (n_ == 4 * j_ + 4), 1.0, np.where((n_ >= 4 * j_ + 1) & (n_ <= 4 * j_ + 3), 2.0, 0.0))
    msel = msel.reshape(4, 128, 128).transpose(1, 0, 2).astype(np.float32)
    r = np.arange(128)[:, None, None]
    u = np.arange(16)[None, :, None]
    i = np.arange(128)[None, None, :]
    maskc = (r <= 8 * u + np.floor_divide(i - 15, 16)).astype(np.float32)
    maskr0 = np.ones((128, 128), np.float32)
    maskr0[0, :] = 0
    qb = np.arange(nblk)[:, None, None]
    ii = np.arange(128)[None, :, None]
    jj = np.arange(128)[None, None, :]
    cur = 2 * qb + (ii >= 64)
    forced = (jj == 0) | (jj == cur) | (jj == cur - 1)
    bvalid = jj <= cur
    mulm = (bvalid & ~forced).astype(np.float32)
    addm = np.where(forced, 1e4, np.where(bvalid, 0.0, -1.0)).astype(np.float32)
    jx = np.arange(128)[:, None, None]
    kb = np.arange(nblk)[None, :, None]
    key = np.arange(128)[None, None, :]
    esel = (jx == 2 * kb + key // 64).astype(np.float32)
    return dict(msel=msel, maskc=maskc, maskr0=maskr0, mulm=mulm, addm=addm, esel=esel)


def c_weights(w_ck, w_cv, pos):
    def w2(w):
        t = np.transpose(w, (1, 0, 2))
        t = np.concatenate([t, t], axis=2)
        return np.ascontiguousarray(np.concatenate([t, t], axis=0))
    return dict(wck2=w2(w_ck), wcv2=w2(w_cv), posT=np.ascontiguousarray(pos.T))


def build_mixer_c(seq=SEQ):
    nc = bass.Bass("TRN2", target_bir_lowering=False)
    nsc = seq // 2048
    nblk = seq // 128
    NCOL = 1036
    with ExitStack() as stack:
        S = Sched(nc, stack)
        M = MixCtx(nc, S, stack, NCOL, 1, seq, rope_nh=5)

        def din(name, shape):
            return nc.dram_tensor(name, shape, F32, kind="ExternalInput").ap()
        msel_d, maskc_d, maskr0_d = din("msel", [128, 4, 128]), din("maskc", [128, 16, 128]), din("maskr0", [128, 128])
        mulm_d, addm_d, esel_d = din("mulm", [nblk, 128, 128]), din("addm", [nblk, 128, 128]), din("esel", [128, nblk, 128])
        wck_d, wcv_d, posT_d = din("wck2", [128, 32, 128]), din("wcv2", [128, 32, 128]), din("posT", [64, 32])

        big = S.sb([128, 2048], F32, "bigst")

        def load_const(dst, src_ap, n, eng="dve"):
            for o in range(0, n, 2048):
                m = min(2048, n - o)
                S.op("sp", lambda e, o=o, m=m: e.dma_start(out=big[:, 0:m], in_=src_ap[:, o:o + m]), w=[big], dma=True)
                S.op(eng, lambda e, o=o, m=m: e.tensor_copy(dst[:, o:o + m], big[:, 0:m]), r=[big], w=[dst_t])

        Msel = S.sb([128, 4, 128], BF16, "Msel")
        dst_t = Msel
        load_const(Msel[:].rearrange("p c j -> p (c j)"), msel_d.rearrange("p c j -> p (c j)"), 512)
        maskC = S.sb([128, 16, 128], BF16, "maskC")
        dst_t = maskC
        load_const(maskC[:].rearrange("p c j -> p (c j)"), maskc_d.rearrange("p c j -> p (c j)"), 2048)
        maskR0 = S.sb([128, 128], BF16, "maskR0")
        dst_t = maskR0
        load_const(maskR0[:], maskr0_d, 128)
        Esel = S.sb([128, nblk, 128], BF16, "Esel")
        dst_t = Esel
        load_const(Esel[:].rearrange("p c j -> p (c j)"), esel_d.rearrange("p c j -> p (c j)"), nblk * 128)
        Wck = S.sb([128, 32, 128], BF16, "Wck")
        dst_t = Wck
        load_const(Wck[:].rearrange("p c j -> p (c j)"), wck_d.rearrange("p c j -> p (c j)"), 4096)
        Wcv = S.sb([128, 32, 128], BF16, "Wcv")
        dst_t = Wcv
        load_const(Wcv[:].rearrange("p c j -> p (c j)"), wcv_d.rearrange("p c j -> p (c j)"), 4096)
        posf = S.sb([128, 32], F32, "posf")
        pos2 = S.sb([128, 32], BF16, "pos2")
        S.op("pool", lambda e: e.memset(posf[:], 0.0), w=[posf])
        S.op("sp", lambda e: e.dma_start(out=posf[0:64, :], in_=posT_d), w=[posf], dma=True)
        S.op("dve", lambda e: e.tensor_copy(pos2[:], posf[:]), r=[posf], w=[pos2])
        cconst = S.sb([128, 2], F32, "cconst")
        cb0 = M.pbanks.next()
        for xi, Wc in enumerate((Wck, Wcv)):
            for j in range(32):
                S.op("pe", lambda e, xi=xi, Wc=Wc, j=j: e.matmul(cb0[:, xi:xi + 1], lhsT=Wc[:, j, :], rhs=pos2[:, j:j + 1],
                                                                 start=(j == 0), stop=(j == 31)), r=[Wc, pos2], w=[cb0])
        S.op("dve", lambda e: e.tensor_copy(cconst[:], cb0[:, 0:2]), r=[cb0], w=[cconst])

        KS = S.sb([128, nblk, 128], BF16, "KS")
        KSb = [Buf(f"KS{j}") for j in range(nblk)]
        VS = S.sb([128, nblk, 128], BF16, "VS")
        VSb = [Buf(f"VS{j}") for j in range(nblk)]
        NR = 20
        KW = S.sb([128, NR, 128], BF16, "KW")
        KWb = [Buf(f"KW{j}") for j in range(NR)]
        VW = S.sb([128, NR, 128], BF16, "VW")
        VWb = [Buf(f"VW{j}") for j in range(NR)]
        S.op("pool", lambda e: e.memset(VS[:], 1.0), w=VSb)
        S.op("pool", lambda e: e.memset(VW[:], 1.0), w=VWb)
        kcT = S.sb([128, 2048], BF16, "kcT")
        vcT = S.sb([128, 2048], BF16, "vcT")
        kcTb = [Buf(f"kcT{j}") for j in range(16)]
        vcTb = [Buf(f"vcT{j}") for j in range(16)]
        kcmpT = S.sb([128, 512], BF16, "kcmpT")
        kcmpb = [Buf(f"kcmp{j}") for j in range(4)]
        Vcmp = S.sb([128, 4, 128], BF16, "Vcmp")
        Vcmpb = [Buf(f"Vcmp{j}") for j in range(4)]
        S.op("pool", lambda e: e.memset(Vcmp[:], 1.0), w=Vcmpb)
        carry = S.sb([128, 2], F32, "carry")
        S.op("pool", lambda e: e.memset(carry[:], 0.0), w=[carry])
        csb = S.sb([128, 512], F32, "csb")
        ctmp = S.sb([128, 2, 128], F32, "ctmp")
        rkcs = Rot([S.sb([128, 128], BF16, f"rkc{i}") for i in range(2)])
        rvcs = Rot([S.sb([128, 128], BF16, f"rvc{i}") for i in range(2)])
        rkss = Rot([S.sb([128, 128], BF16, f"rks{i}") for i in range(2)])
        rkws = Rot([S.sb([128, 128], BF16, f"rkw{i}") for i in range(2)])
        for t in rkcs.tiles + rvcs.tiles:
            S.op("pool", lambda e, t=t: e.memset(t[:], 0.0), w=[t])
        rqzs = Rot([S.sb([128, 4, 128], BF16, f"rqz{i}") for i in range(2)])
        for t in rqzs.tiles:
            S.op("pool", lambda e, t=t: e.memset(t[:], 0.0), w=[t])
        qzs = Rot([S.sb([128, 4, 128], BF16, f"qz{i}") for i in range(2)])
        szs = Rot([S.sb([128, 256], F32, f"sz{i}") for i in range(2)])
        gts = Rot([S.sb([128, 12], F32, f"gt{i}") for i in range(2)])
        ptc = Rot([S.sb([128, 512], BF16, f"ptc{i}") for i in range(5)])
        mulms = Rot([S.sb([128, 128], F32, f"mulm{i}") for i in range(2)])
        addms = Rot([S.sb([128, 128], F32, f"addm{i}") for i in range(2)])
        sc0 = S.sb([128, 128], F32, "sc0")
        sc1 = S.sb([128, 128], F32, "sc1")
        wk = S.sb([128, 128], F32, "wk")
        sm = S.sb([128, 32], F32, "sm")
        selneg = S.sb([128, 128], BF16, "selneg")
        selnegT = S.sb([128, 128], BF16, "selnegT")
        oT = S.sb([128, 512], F32, "oT")
        oacc = S.sb([128, 256], F32, "oacc")
        coef = S.sb([128, 8], F32, "coef")
        ogts = Rot([S.sb([128, 256], F32, f"ogt{i}") for i in range(2)])

        for sc in range(nsc):
            M.load_xT(sc)
            for jb in range(16):
                qb = sc * 16 + jb
                slot = qb % NR
                tsl = slice(jb * 128, (jb + 1) * 128)
                bank = M.pbanks.next()
                M.project(tsl, 0, 512, bank)
                cs_t = M.load_cs(0, qb)
                rkc, rvc, rks, rkw = rkcs.next(), rvcs.next(), rkss.next(), rkws.next()
                M.rope(bank, 0, 5, cs_t, [(slice(0, 1), rkc[:, 0:64].unsqueeze(1), [rkc]),
                                           (slice(1, 3), rks[:].rearrange("p (h d) -> p h d", h=2), [rks]),
                                           (slice(3, 5), rkw[:].rearrange("p (h d) -> p h d", h=2), [rkw])])
                S.op("act", lambda e, rvc=rvc, bank=bank: e.copy(out=rvc[:, 0:64], in_=bank[:, 320:384]), r=[bank], w=[rvc])
                S.op("act", lambda e, qb=qb, bank=bank: e.copy(out=VS[:, qb, 0:64], in_=bank[:, 384:448]), r=[bank], w=[VSb[qb]])
                S.op("act", lambda e, slot=slot, bank=bank: e.copy(out=VW[:, slot, 0:64], in_=bank[:, 448:512]), r=[bank], w=[VWb[slot]])
                tb = M.tbanks.next()
                tbv = tb.t[:].bitcast(BF16)
                for i, t in enumerate((rkc, rvc, rks, rkw)):
                    S.op("pe", lambda e, i=i, t=t, tbv=tbv: e.transpose(tbv[:, i * 128:(i + 1) * 128], t[:], M.ident[:]),
                         r=[t, M.ident], w=[tb])
                S.op("dve", lambda e, tsl=tsl, tbv=tbv: e.tensor_copy(kcT[:, tsl], tbv[:, 0:128]), r=[tb], w=[kcTb[jb]])
                S.op("dve", lambda e, tsl=tsl, tbv=tbv: e.tensor_copy(vcT[:, tsl], tbv[:, 128:256]), r=[tb], w=[vcTb[jb]])
                S.op("act", lambda e, qb=qb, tbv=tbv: e.copy(out=KS[:, qb, :], in_=tbv[:, 256:384]), r=[tb], w=[KSb[qb]])
                S.op("act", lambda e, slot=slot, tbv=tbv: e.copy(out=KW[:, slot, :], in_=tbv[:, 384:512]), r=[tb], w=[KWb[slot]])
            cb = M.pbanks.next()
            for xi, (Wc, srcT, srcb) in enumerate(((Wck, kcT, kcTb), (Wcv, vcT, vcTb))):
                for grp in range(2):
                    o = (xi * 2 + grp) * 128
                    for j in range(16):
                        S.op("pe", lambda e, Wc=Wc, srcT=srcT, grp=grp, j=j, o=o: e.matmul(
                            cb[:, o:o + 128], lhsT=Wc[:, grp * 16 + j, :], rhs=srcT[:, j:2048:16], start=(j == 0), stop=(j == 15)),
                            r=[Wc] + srcb, w=[cb])
            S.op("act", lambda e: e.copy(out=csb[:], in_=cb[:]), r=[cb], w=[csb])
            for xi in range(2):
                a0, a1 = csb[:, (2 * xi) * 128:(2 * xi + 1) * 128], csb[:, (2 * xi + 1) * 128:(2 * xi + 2) * 128]
                S.op("dve", lambda e, xi=xi, a0=a0, a1=a1: e.tensor_tensor(out=ctmp[:, xi, 1:128], in0=a0[:, 0:127], in1=a1[:, 1:128], op=ALU.add),
                     r=[csb], w=[ctmp])
                S.op("dve", lambda e, xi=xi, a1=a1: e.tensor_tensor(out=ctmp[:, xi, 0:1], in0=carry[:, xi:xi + 1], in1=a1[:, 0:1], op=ALU.add),
                     r=[csb, carry], w=[ctmp])
                S.op("dve", lambda e, xi=xi, a0=a0: e.tensor_copy(carry[:, xi:xi + 1], a0[:, 127:128]), r=[csb, ctmp], w=[carry])
            S.op("dve", lambda e: e.tensor_scalar(out=kcmpT[:, sc * 128:(sc + 1) * 128], in0=ctmp[:, 0, :], scalar1=cconst[:, 0:1], scalar2=None,
                                                  op0=ALU.add), r=[ctmp, cconst], w=[kcmpb[sc]])
            S.op("dve", lambda e: e.tensor_scalar(out=ctmp[:, 1, :], in0=ctmp[:, 1, :], scalar1=cconst[:, 1:2], scalar2=None, op0=ALU.add),
                 r=[ctmp, cconst], w=[ctmp])
            fbv = M.fbank.next()
            S.op("pe", lambda e, fbv=fbv: e.transpose(fbv[:, 0:128], ctmp[:, 1, :], M.identf[:]), r=[ctmp, M.identf], w=[fbv])
            S.op("act", lambda e, fbv=fbv: e.copy(out=Vcmp[:, sc, 0:64], in_=fbv[:, 0:64]), r=[fbv], w=[Vcmpb[sc]])

            for jb in range(16):
                qb = sc * 16 + jb
                tsl = slice(jb * 128, (jb + 1) * 128)
                bank = M.pbanks.next()
                M.project(tsl, 512, 268, bank)
                cs_t = M.load_cs(0, qb)
                rqz = rqzs.next()
                v8 = rqz[:].rearrange("p h (g d) -> p (h g) d", g=2)
                M.rope(bank, 0, 4, cs_t, [(slice(0, 4, 2), v8[:, 0::4, :], [rqz]), (slice(1, 4, 2), v8[:, 3::4, :], [rqz])])
                gt = gts.next()
                S.op("act", lambda e, gt=gt, bank=bank: e.activation(out=gt[:], in_=bank[:, 256:268], func=AF.Exp, scale=-1.0), r=[bank], w=[gt])
                S.op("dve", lambda e, gt=gt: e.tensor_scalar(out=gt[:], in0=gt[:], scalar1=1.0, scalar2=None, op0=ALU.add), r=[gt], w=[gt])
                S.op("dve", lambda e, gt=gt: e.reciprocal(out=gt[:], in_=gt[:]), r=[gt], w=[gt])
                bank2 = M.pbanks.next()
                M.project(tsl, 780, 256, bank2)
                sz = szs.next()
                M.silu_from_bank(bank2, 0, 256, sz[:], [sz])
                tb = M.tbanks.next()
                tbv = tb.t[:].bitcast(BF16)
                for i in range(4):
                    S.op("pe", lambda e, i=i, rqz=rqz, tbv=tbv: e.transpose(tbv[:, i * 128:(i + 1) * 128], rqz[:, i, :], M.ident[:]),
                         r=[rqz, M.ident], w=[tb])
                qz = qzs.next()
                S.op("dve", lambda e, qz=qz, tbv=tbv: e.tensor_copy(qz[:], tbv[:, 0:512].rearrange("p (h t) -> p h t", h=4)),
                     r=[tb], w=[qz])
                qzf = qz[:].rearrange("p h t -> p (h t)")
                pts_c = []
                for c in range(sc + 1):
                    sbk = M.sbanks.next()
                    S.op("pe", lambda e, c=c, sbk=sbk, qzf=qzf: e.matmul(sbk[:], lhsT=kcmpT[:, c * 128:(c + 1) * 128], rhs=qzf, start=True, stop=True),
                         r=[qz, kcmpb[c]], w=[sbk])
                    PT = ptc.next()
                    S.op("act", lambda e, PT=PT, sbk=sbk: e.activation(out=PT[:], in_=sbk[:], func=AF.Exp, scale=0.125), r=[sbk], w=[PT])
                    pv = PT[:].rearrange("p (h t) -> p h t", h=4)
                    if c == sc:
                        mv = maskC[:, jb, :].unsqueeze(1).broadcast_to([128, 4, 128])
                        S.op("dve", lambda e, pv=pv, mv=mv: e.tensor_tensor(out=pv, in0=pv, in1=mv, op=ALU.mult), r=[PT, maskC], w=[PT])
                    if c == 0:
                        mv0 = maskR0[:].unsqueeze(1).broadcast_to([128, 4, 128])
                        S.op("dve", lambda e, pv=pv, mv0=mv0: e.tensor_tensor(out=pv, in0=pv, in1=mv0, op=ALU.mult), r=[PT, maskR0], w=[PT])
                    pts_c.append(PT)
                ob = M.obank.next()
                for c, PT in enumerate(pts_c):
                    S.op("pe", lambda e, c=c, PT=PT, ob=ob: e.matmul(ob[:], lhsT=Vcmp[:, c, :], rhs=PT[:], start=(c == 0), stop=(c == sc)),
                         r=[PT, Vcmpb[c]], w=[ob])
                ib = M.pbanks.next()
                for h in range(4):
                    for c, PT in enumerate(pts_c):
                        S.op("pe", lambda e, h=h, c=c, PT=PT, ib=ib: e.matmul(ib[:, h * 128:(h + 1) * 128], lhsT=PT[:, h * 128:(h + 1) * 128],
                                                                            rhs=Msel[:, c, :], start=(c == 0), stop=(c == sc)),
                             r=[PT, Msel], w=[ib])
                S.op("act", lambda e, ob=ob: e.copy(out=oT[:], in_=ob[:]), r=[ob], w=[oT])

                def fin(br, first, gt=gt):
                    fb = M.fbank.next()
                    for h in range(4):
                        S.op("pe", lambda e, h=h, fb=fb: e.transpose(fb[:, h * 128:(h + 1) * 128], oT[:, h * 128:(h + 1) * 128], M.identf[:]),
                             r=[oT, M.identf], w=[fb])
                    fbv_ = fb[:].rearrange("p (h c) -> p h c", h=4)
                    S.op("dve", lambda e, fbv_=fbv_: e.tensor_scalar(out=coef[:, 0:4].unsqueeze(2), in0=fbv_[:, :, 64:65], scalar1=1e-30, scalar2=None,
                                                                     op0=ALU.max), r=[fb], w=[coef])
                    S.op("dve", lambda e: e.reciprocal(out=coef[:, 0:4], in_=coef[:, 0:4]), r=[coef], w=[coef])
                    S.op("dve", lambda e, br=br, gt=gt: e.tensor_tensor(out=coef[:, 4:8], in0=coef[:, 0:4], in1=gt[:, br * 4:(br + 1) * 4], op=ALU.mult),
                         r=[coef, gt], w=[coef])
                    for h in range(4):
                        if first:
                            S.op("dve", lambda e, h=h, fb=fb: e.tensor_scalar(out=oacc[:, h * 64:(h + 1) * 64], in0=fb[:, h * 128:h * 128 + 64],
                                                                               scalar1=coef[:, 4 + h:5 + h], scalar2=None, op0=ALU.mult),
                                 r=[fb, coef], w=[oacc])
                        else:
                            S.op("dve", lambda e, h=h, fb=fb: e.scalar_tensor_tensor(
                                out=oacc[:, h * 64:(h + 1) * 64], in0=fb[:, h * 128:h * 128 + 64], scalar=coef[:, 4 + h:5 + h],
                                in1=oacc[:, h * 64:(h + 1) * 64], op0=ALU.mult, op1=ALU.add), r=[fb, coef, oacc], w=[oacc])

                fin(0, True)
                ibv = ib[:].rearrange("p (h j) -> p h j", h=4)
                S.op("dve", lambda e, ibv=ibv: e.reduce_sum(out=sm[:, 0:4], in_=ibv, axis=AX.X), r=[ib], w=[sm])
                S.op("dve", lambda e: e.tensor_scalar(out=sm[:, 0:4], in0=sm[:, 0:4], scalar1=1e-30, scalar2=None, op0=ALU.max), r=[sm], w=[sm])
                S.op("dve", lambda e: e.reciprocal(out=sm[:, 4:8], in_=sm[:, 0:4]), r=[sm], w=[sm])
                S.op("dve", lambda e, ib=ib: e.tensor_scalar(out=sc0[:], in0=ib[:, 0:128], scalar1=sm[:, 4:5], scalar2=None, op0=ALU.mult),
                     r=[ib, sm], w=[sc0])
                for h in range(1, 4):
                    S.op("dve", lambda e, h=h, ib=ib: e.scalar_tensor_tensor(out=sc0[:], in0=ib[:, h * 128:(h + 1) * 128], scalar=sm[:, 4 + h:5 + h],
                                                                            in1=sc0[:], op0=ALU.mult, op1=ALU.add), r=[ib, sm, sc0], w=[sc0])
                mm, am = mulms.next(), addms.next()
                S.op("sp", lambda e, mm=mm, qb=qb: e.dma_start(out=mm[:], in_=mulm_d[qb]), w=[mm], dma=True)
                S.op("sp", lambda e, am=am, qb=qb: e.dma_start(out=am[:], in_=addm_d[qb]), w=[am], dma=True)
                S.op("dve", lambda e, mm=mm: e.tensor_tensor(out=sc1[:], in0=sc0[:], in1=mm[:], op=ALU.mult), r=[sc0, mm], w=[sc1])
                S.op("dve", lambda e, am=am: e.tensor_tensor(out=sc1[:], in0=sc1[:], in1=am[:], op=ALU.add), r=[sc1, am], w=[sc1])
                S.op("dve", lambda e: e.max(out=sm[:, 8:16], in_=sc1[:]), r=[sc1], w=[sm])
                S.op("dve", lambda e: e.match_replace(out=wk[:], in_to_replace=sm[:, 8:16], in_values=sc1[:], imm_value=-1e30), r=[sc1, sm], w=[wk])
                S.op("dve", lambda e: e.max(out=sm[:, 16:24], in_=wk[:]), r=[wk], w=[sm])
                S.op("dve", lambda e: e.tensor_scalar(out=selneg[:], in0=sc1[:], scalar1=sm[:, 23:24], scalar2=-30000.0, op0=ALU.is_lt, op1=ALU.mult),
                     r=[sc1, sm], w=[selneg])
                tb2 = M.tbanks.next()
                tb2v = tb2.t[:].bitcast(BF16)
                S.op("pe", lambda e, tb2v=tb2v: e.transpose(tb2v[:, 0:128], selneg[:], M.ident[:]), r=[selneg, M.ident], w=[tb2])
                S.op("act", lambda e, tb2v=tb2v: e.copy(out=selnegT[:], in_=tb2v[:, 0:128]), r=[tb2], w=[selnegT])
                ob = M.obank.next()
                for kb in range(qb + 1):
                    sbk = M.sbanks.next()
                    S.op("pe", lambda e, kb=kb, sbk=sbk, qzf=qzf: e.matmul(sbk[:], lhsT=KS[:, kb, :], rhs=qzf, start=True, stop=True),
                         r=[qz, KSb[kb]], w=[sbk])
                    if kb < qb:
                        for h in range(4):
                            S.op("pe", lambda e, kb=kb, h=h, sbk=sbk: e.matmul(sbk[:, h * 128:(h + 1) * 128], lhsT=Esel[:, kb, :], rhs=selnegT[:],
                                                                              start=False, stop=True, skip_group_check=True),
                                 r=[Esel, selnegT], w=[sbk])
                    PT = M.pts.next()
                    S.op("act", lambda e, PT=PT, sbk=sbk: e.activation(out=PT[:], in_=sbk[:], func=AF.Exp, scale=0.125), r=[sbk], w=[PT])
                    if kb == qb:
                        pv = PT[:].rearrange("p (h t) -> p h t", h=4)
                        mv = M.mask[:, 0, :].unsqueeze(1).broadcast_to([128, 4, 128])
                        S.op("dve", lambda e, pv=pv, mv=mv: e.tensor_tensor(out=pv, in0=pv, in1=mv, op=ALU.mult), r=[PT, M.mask], w=[PT])
                    S.op("pe", lambda e, kb=kb, PT=PT, ob=ob: e.matmul(ob[:], lhsT=VS[:, kb, :], rhs=PT[:], start=(kb == 0), stop=(kb == qb)),
                         r=[PT, VSb[kb]], w=[ob])
                S.op("act", lambda e, ob=ob: e.copy(out=oT[:], in_=ob[:]), r=[ob], w=[oT])
                fin(1, False)
                blocks = []
                for kb in range(max(0, qb - 4), qb + 1):
                    mk = M.mask[:, 0, :] if kb == qb else (M.mask[:, 2, :] if kb == qb - 4 else None)
                    sl_ = kb % NR
                    blocks.append((KW[:, sl_, :], VW[:, sl_, :], mk, [KWb[sl_], VWb[sl_]]))
                ob = M.obank.next()
                shared_attend(S, M, qz, blocks, ob)
                S.op("act", lambda e, ob=ob: e.copy(out=oT[:], in_=ob[:]), r=[ob], w=[oT])
                fin(2, False)
                ogt = ogts.next()
                S.op("dve", lambda e, ogt=ogt, sz=sz: e.tensor_tensor(out=ogt[:], in0=oacc[:], in1=sz[:], op=ALU.mult), r=[oacc, sz], w=[ogt])
                rows = slice(qb * 128, (qb + 1) * 128)
                S.op("sp", lambda e, ogt=ogt, rows=rows: e.dma_start(out=M.og[rows, :], in_=ogt[:]), r=[ogt], dma=True, out=True)
        S.emit()
    return nc


_PROGS = {}


def _prog(name, builder):
    if name not in _PROGS:
        _PROGS[name] = builder()
    return _PROGS[name]


def _run_mixer(kind, x, w_in, extra=None):
    nc = _prog("mix_" + kind, {"a": build_mixer_a, "b": build_mixer_b, "c": build_mixer_c}[kind])
    in_maps = mixer_in_maps(kind, x, w_in, SEQ, BATCH, extra)
    res = run_bass_kernel_spmd(nc, in_maps, core_ids=list(range(NCORES)))
    return gather_og(res, SEQ, BATCH)


def _run_tail(og, x, p_i, w_out, w_g, w_p, g, b):
    nc = _prog("tail", build_tail)
    nt = BATCH * SEQ // NCORES
    ogf, xf, pf = og.reshape(-1, D_MODEL), x.reshape(-1, D_MODEL), p_i.reshape(-1, 256)
    base = dict(w_out=np.ascontiguousarray(w_out), w_g=np.ascontiguousarray(w_g), w_p=np.ascontiguousarray(w_p),
                ln_g=np.ascontiguousarray(g[None, :]), ln_b=np.ascontiguousarray(b[None, :]), ident=np.eye(128, dtype=np.float32))
    in_maps = [dict(base, og=np.ascontiguousarray(ogf[c * nt:(c + 1) * nt]), x=np.ascontiguousarray(xf[c * nt:(c + 1) * nt]),
                    p=np.ascontiguousarray(pf[c * nt:(c + 1) * nt])) for c in range(NCORES)]
    res = run_bass_kernel_spmd(nc, in_maps, core_ids=list(range(NCORES)))
    return np.concatenate([res.results[c]["y"] for c in range(NCORES)], axis=0).reshape(BATCH, SEQ, D_MODEL)


def kernel(x, p, a_w_in, a_w_out, b_w_in, b_sinks, b_w_out, c_w_in, c_w_ck, c_w_cv, c_pos, c_w_out, ln_g, ln_b,
           ple_w_proj, ple_w_gate):
    f = lambda a: np.asarray(a, dtype=np.float32)
    x, p = f(x), f(p)
    a_w_in, a_w_out, b_w_in, b_sinks, b_w_out = f(a_w_in), f(a_w_out), f(b_w_in), f(b_sinks), f(b_w_out)
    c_w_in, c_w_ck, c_w_cv, c_pos, c_w_out = f(c_w_in), f(c_w_ck), f(c_w_cv), f(c_pos), f(c_w_out)
    ln_g, ln_b, ple_w_proj, ple_w_gate = f(ln_g), f(ln_b), f(ple_w_proj), f(ple_w_gate)
    for i in range(DEPTH):
        j, kind = i // 3, i % 3
        if kind == 0:
            og = _run_mixer("a", x, a_w_in[j])
            w_out = a_w_out[j]
        elif kind == 1:
            sk = b_sinks[j]
            og = _run_mixer("b", x, b_w_in[j], extra=lambda b, hg: dict(sinks=np.ascontiguousarray(sk[None, hg * 4:(hg + 1) * 4])))
            w_out = b_w_out[j]
        else:
            cc = c_consts(SEQ)
            cw = c_weights(c_w_ck[j], c_w_cv[j], c_pos[j])
            og = _run_mixer("c", x, c_w_in[j], extra=lambda b, hg: dict(cc, **cw))
            w_out = c_w_out[j]
        x = _run_tail(og, x, p[i], w_out, ple_w_gate[i], ple_w_proj[i], ln_g[i], ln_b[i])
    return x.astype(np.float32)
```
